# Optimizing a Trainium2 kernel written in Bass

```python
import math
import jax, jax.numpy as jnp
from jax import lax
import numpy as np

D_MODEL = 1024
BATCH = 16
SEQ = 2048
DEPTH = 1
DEC_BATCH = 16
DEC_SEQ = 4096
PAST_LEN = 128

N_HEADS = 8
QK_NOPE_DIM = 64
QK_ROPE_DIM = 32
QK_DIM = QK_NOPE_DIM + QK_ROPE_DIM
V_HEAD_DIM = 64
Q_LORA_RANK = 256
KV_LORA_RANK = 128
ATTN_WIDTH = N_HEADS * V_HEAD_DIM
ROPE_THETA = 10000.0
Q_BLOCK = 128
LRU_WIDTH = D_MODEL
LRU_BLOCKS = 8
LRU_BLOCK_DIM = LRU_WIDTH // LRU_BLOCKS
CONV_WIDTH = 4
LRU_C = 8.0
N_DIRECTIONS = 2
D_FF = 2816
FFN_RESIDUAL = 0.5
N_SUBLAYERS = 3
EPS = 1e-6
SPLIT_POINTS = (
    Q_LORA_RANK,
    Q_LORA_RANK + KV_LORA_RANK,
    Q_LORA_RANK + KV_LORA_RANK + QK_ROPE_DIM,
    Q_LORA_RANK + KV_LORA_RANK + QK_ROPE_DIM + LRU_WIDTH,
    Q_LORA_RANK + KV_LORA_RANK + QK_ROPE_DIM + 2 * LRU_WIDTH,
    Q_LORA_RANK + KV_LORA_RANK + QK_ROPE_DIM + 2 * LRU_WIDTH + D_MODEL,
)
COMBINED_WIDTH = Q_LORA_RANK + KV_LORA_RANK + QK_ROPE_DIM + 2 * LRU_WIDTH + 2 * D_MODEL

kernel_name = "hybrid_mla_rglru_macaron_encoder"


def rmsnorm(x, g):
    xf = x.astype(jnp.float32)
    y = xf * lax.rsqrt(jnp.mean(xf * xf, axis=-1, keepdims=True) + EPS)
    return (y * g.astype(jnp.float32)).astype(x.dtype)


def rope_tables(seq_len):
    inv = 1.0 / (ROPE_THETA ** (jnp.arange(0, QK_ROPE_DIM, 2, dtype=jnp.float32) / QK_ROPE_DIM))
    ang = jnp.arange(seq_len, dtype=jnp.float32)[:, None] * inv[None, :]
    return jnp.cos(ang), jnp.sin(ang)


def apply_rope(x, cos, sin):
    x1, x2 = jnp.split(x.astype(jnp.float32), 2, axis=-1)
    c = cos[None, :, None, :]
    s = sin[None, :, None, :]
    return jnp.concatenate([x1 * c - x2 * s, x2 * c + x1 * s], axis=-1).astype(x.dtype)


def swiglu(h, w_gu, w_down):
    g, u = jnp.split(h @ w_gu, 2, axis=-1)
    return (jax.nn.silu(g) * u) @ w_down


def bidir_attention(q, k, v):
    b, s = q.shape[0], q.shape[1]
    nb = s // Q_BLOCK
    qb = q.reshape(b, nb, Q_BLOCK, N_HEADS, QK_DIM).transpose(1, 0, 2, 3, 4)
    scale = QK_DIM ** -0.5

    def one_block(qblk):
        sc = jnp.einsum("bqhd,bkhd->bhqk", qblk, k, preferred_element_type=jnp.float32) * scale
        p = jax.nn.softmax(sc, axis=-1)
        return jnp.einsum("bhqk,bkhd->bqhd", p.astype(v.dtype), v)

    o = lax.map(one_block, qb)
    return o.transpose(1, 0, 2, 3, 4).reshape(b, s, ATTN_WIDTH)


def mla_branch(c_q, c_kv, k_rope, cos, sin, g_q_norm, g_kv_norm, w_q_b, w_kv_b, w_attn_o):
    b, s = c_q.shape[0], c_q.shape[1]
    q = (rmsnorm(c_q, g_q_norm) @ w_q_b).reshape(b, s, N_HEADS, QK_DIM)
    q = jnp.concatenate([q[..., :QK_NOPE_DIM], apply_rope(q[..., QK_NOPE_DIM:], cos, sin)], axis=-1)
    kv = (rmsnorm(c_kv, g_kv_norm) @ w_kv_b).reshape(b, s, N_HEADS, QK_NOPE_DIM + V_HEAD_DIM)
    k_nope, v = kv[..., :QK_NOPE_DIM], kv[..., QK_NOPE_DIM:]
    k_r = apply_rope(k_rope[:, :, None, :], cos, sin)
    k = jnp.concatenate([k_nope, jnp.broadcast_to(k_r, (b, s, N_HEADS, QK_ROPE_DIM))], axis=-1)
    return bidir_attention(q, k, v) @ w_attn_o


def centred_depthwise_conv(x, w, bias):
    s = x.shape[1]
    left = (CONV_WIDTH - 1) // 2
    right = CONV_WIDTH - 1 - left
    xp = jnp.pad(x, ((0, 0), (left, right), (0, 0)))
    out = bias + xp[:, 0:s, :] * w[0]
    for j in range(1, CONV_WIDTH):
        out = out + xp[:, j:j + s, :] * w[j]
    return out


def block_diag_linear(x, w, bias):
    b, s = x.shape[0], x.shape[1]
    xb = x.reshape(b, s, LRU_BLOCKS, LRU_BLOCK_DIM)
    return jnp.einsum("bsnd,nde->bsne", xb, w).reshape(b, s, LRU_WIDTH) + bias


def rglru_direction(x, w_a, b_a, w_i, b_i, lam, reverse):
    r = jax.nn.sigmoid(block_diag_linear(x, w_a, b_a).astype(jnp.float32))
    i = jax.nn.sigmoid(block_diag_linear(x, w_i, b_i).astype(jnp.float32))
    log_a = -LRU_C * r * jax.nn.softplus(-lam.astype(jnp.float32))
    a = jnp.exp(log_a)
    u = jnp.sqrt(-jnp.expm1(2.0 * log_a)) * (i * x.astype(jnp.float32))

    def combine(e1, e2):
        a1, b1 = e1
        a2, b2 = e2
        return a1 * a2, a2 * b1 + b2

    _, h = lax.associative_scan(combine, (a, u), axis=1, reverse=reverse)
    return h


def encoder_layer(x, c, cos, sin, w_ada, b_ada, g_pre, g_post, w_ffn1_gu, w_ffn1_down, w_in,
                  g_q_norm, g_kv_norm, w_q_b, w_kv_b, w_attn_o, conv_w, conv_b,
                  lru_w_a, lru_b_a, lru_w_i, lru_b_i, lru_lambda, w_lru_o, w_out,
                  w_ffn2_gu, w_ffn2_down):
    b = x.shape[0]
    mod = (jax.nn.silu(c) @ w_ada + b_ada).reshape(b, N_SUBLAYERS, 3, D_MODEL)
    shift = mod[:, :, 0, None, :]
    scl = mod[:, :, 1, None, :]
    gate = mod[:, :, 2, None, :]

    h = rmsnorm(x, g_pre[0]) * (1.0 + scl[:, 0]) + shift[:, 0]
    f = rmsnorm(swiglu(h, w_ffn1_gu, w_ffn1_down), g_post[0])
    x = x + FFN_RESIDUAL * gate[:, 0] * f

    h = rmsnorm(x, g_pre[1]) * (1.0 + scl[:, 1]) + shift[:, 1]
    z = h @ w_in
    c_q, c_kv, k_rope, x_lru, y_lru, g_att, g_lru = jnp.split(z, SPLIT_POINTS, axis=-1)
    o_att = mla_branch(c_q, c_kv, k_rope, cos, sin, g_q_norm, g_kv_norm, w_q_b, w_kv_b, w_attn_o)
    xc = centred_depthwise_conv(x_lru, conv_w, conv_b)
    h_lru = (rglru_direction(xc, lru_w_a[0], lru_b_a[0], lru_w_i[0], lru_b_i[0], lru_lambda[0], False)
             + rglru_direction(xc, lru_w_a[1], lru_b_a[1], lru_w_i[1], lru_b_i[1], lru_lambda[1], True))
    o_lru = (h_lru.astype(x.dtype) * jax.nn.gelu(y_lru)) @ w_lru_o
    merged = jax.nn.sigmoid(g_att) * o_att + jax.nn.sigmoid(g_lru) * o_lru
    m = rmsnorm(merged @ w_out, g_post[1])
    x = x + gate[:, 1] * m

    h = rmsnorm(x, g_pre[2]) * (1.0 + scl[:, 2]) + shift[:, 2]
    f = rmsnorm(swiglu(h, w_ffn2_gu, w_ffn2_down), g_post[2])
    return x + FFN_RESIDUAL * gate[:, 2] * f


def setup_inputs(seed: int = 0) -> dict:
    key = jax.random.key(seed)
    ks = jax.random.split(key, 32)
    f32 = jnp.float32

    def normal(k, shape, scale):
        return jax.random.normal(k, shape, f32) * scale

    u = jax.random.uniform(ks[24], (DEPTH, N_DIRECTIONS, LRU_WIDTH), f32, minval=0.9, maxval=0.999)
    a_base = u ** (1.0 / LRU_C)
    lru_lambda = jnp.log(a_base) - jnp.log1p(-a_base)
    return {
        "x_prompt": normal(ks[0], (BATCH, SEQ, D_MODEL), 1.0),
        "x_sample": normal(ks[1], (DEC_BATCH, DEC_SEQ, D_MODEL), 1.0),
        "c_prompt": normal(ks[2], (BATCH, D_MODEL), 1.0),
        "c_sample": normal(ks[3], (DEC_BATCH, D_MODEL), 1.0),
        "w_ada": normal(ks[4], (DEPTH, D_MODEL, N_SUBLAYERS * 3 * D_MODEL), 0.5 * D_MODEL ** -0.5),
        "b_ada": normal(ks[5], (DEPTH, N_SUBLAYERS * 3 * D_MODEL), 0.02),
        "g_pre": 1.0 + normal(ks[6], (DEPTH, N_SUBLAYERS, D_MODEL), 0.05),
        "g_post": 1.0 + normal(ks[7], (DEPTH, N_SUBLAYERS, D_MODEL), 0.05),
        "w_ffn1_gu": normal(ks[8], (DEPTH, D_MODEL, 2 * D_FF), D_MODEL ** -0.5),
        "w_ffn1_down": normal(ks[9], (DEPTH, D_FF, D_MODEL), D_FF ** -0.5),
        "w_in": normal(ks[10], (DEPTH, D_MODEL, COMBINED_WIDTH), D_MODEL ** -0.5),
        "g_q_norm": 1.0 + normal(ks[11], (DEPTH, Q_LORA_RANK), 0.05),
        "g_kv_norm": 1.0 + normal(ks[12], (DEPTH, KV_LORA_RANK), 0.05),
        "w_q_b": normal(ks[13], (DEPTH, Q_LORA_RANK, N_HEADS * QK_DIM), Q_LORA_RANK ** -0.5),
        "w_kv_b": normal(ks[14], (DEPTH, KV_LORA_RANK, N_HEADS * (QK_NOPE_DIM + V_HEAD_DIM)), KV_LORA_RANK ** -0.5),
        "w_attn_o": normal(ks[15], (DEPTH, ATTN_WIDTH, D_MODEL), ATTN_WIDTH ** -0.5),
        "conv_w": normal(ks[16], (DEPTH, CONV_WIDTH, LRU_WIDTH), CONV_WIDTH ** -0.5),
        "conv_b": normal(ks[17], (DEPTH, LRU_WIDTH), 0.02),
        "lru_w_a": normal(ks[18], (DEPTH, N_DIRECTIONS, LRU_BLOCKS, LRU_BLOCK_DIM, LRU_BLOCK_DIM), LRU_BLOCK_DIM ** -0.5),
        "lru_b_a": normal(ks[19], (DEPTH, N_DIRECTIONS, LRU_WIDTH), 0.1),
        "lru_w_i": normal(ks[20], (DEPTH, N_DIRECTIONS, LRU_BLOCKS, LRU_BLOCK_DIM, LRU_BLOCK_DIM), LRU_BLOCK_DIM ** -0.5),
        "lru_b_i": normal(ks[21], (DEPTH, N_DIRECTIONS, LRU_WIDTH), 0.1),
        "lru_lambda": lru_lambda,
        "w_lru_o": normal(ks[22], (DEPTH, LRU_WIDTH, D_MODEL), LRU_WIDTH ** -0.5),
        "w_out": normal(ks[23], (DEPTH, D_MODEL, D_MODEL), D_MODEL ** -0.5),
        "w_ffn2_gu": normal(ks[25], (DEPTH, D_MODEL, 2 * D_FF), D_MODEL ** -0.5),
        "w_ffn2_down": normal(ks[26], (DEPTH, D_FF, D_MODEL), D_FF ** -0.5),
    }


def reference(x_prompt, x_sample, c_prompt, c_sample, w_ada, b_ada, g_pre, g_post, w_ffn1_gu,
              w_ffn1_down, w_in, g_q_norm, g_kv_norm, w_q_b, w_kv_b, w_attn_o, conv_w, conv_b,
              lru_w_a, lru_b_a, lru_w_i, lru_b_i, lru_lambda, w_lru_o, w_out, w_ffn2_gu, w_ffn2_down):
    cos_p, sin_p = rope_tables(x_prompt.shape[1])
    cos_s, sin_s = rope_tables(x_sample.shape[1])
    y_prompt = x_prompt
    y_sample = x_sample
    for l in range(DEPTH):
        layer_params = (w_ada[l], b_ada[l], g_pre[l], g_post[l], w_ffn1_gu[l], w_ffn1_down[l], w_in[l],
                        g_q_norm[l], g_kv_norm[l], w_q_b[l], w_kv_b[l], w_attn_o[l], conv_w[l], conv_b[l],
                        lru_w_a[l], lru_b_a[l], lru_w_i[l], lru_b_i[l], lru_lambda[l], w_lru_o[l], w_out[l],
                        w_ffn2_gu[l], w_ffn2_down[l])
        y_prompt = encoder_layer(y_prompt, c_prompt, cos_p, sin_p, *layer_params)
        y_sample = encoder_layer(y_sample, c_sample, cos_s, sin_s, *layer_params)
    return (y_prompt, y_sample)
```

```python
import contextlib
import numpy as np
import concourse.bass as bass
import concourse.mybir as mybir
from concourse.bass_utils import run_bass_kernel_spmd
from concourse.ap import AP

F32, BF16 = mybir.dt.float32, mybir.dt.bfloat16
AF = mybir.ActivationFunctionType
ALU = mybir.AluOpType

D = 1024
T = 512
DFF = 2816
NH = 8
NOC_FF = DFF // 128
EPS = 1e-6
N_CORES = 8


class _Op:
    __slots__ = ("eng", "fn", "deps", "dma_waits", "signal", "count", "dma_key", "dma_count", "waits")

    def __init__(self, eng, fn, dma_key):
        self.eng = eng
        self.fn = fn
        self.deps = []
        self.dma_waits = []
        self.signal = False
        self.count = 0
        self.dma_key = dma_key
        self.dma_count = 0
        self.waits = []


class Prog:
    ENGS = ("pe", "act", "dve", "pool", "sp")

    def __init__(self, nc, stack):
        self.nc = nc
        self.stack = stack
        self.esem = {e: stack.enter_context(nc.semaphore("s_" + e)) for e in ("pe", "act", "dve", "pool")}
        self.ecount = {e: 0 for e in self.esem}
        self.dsem = {}
        self.dcount = {}
        self.last_write = {}
        self.readers = {}
        self.waited = {e: {} for e in self.ENGS}
        self.ops = []
        self.nblocks = 0

    def add(self, eng, fn, r=(), w=(), dma_key=None):
        op = _Op(eng, fn, dma_key)
        r = list(r)
        w = list(w) + [k for k in r if isinstance(k, tuple) and k[0] == "ps" and k not in w]
        deps = []
        for k in r:
            lw = self.last_write.get(k)
            if lw is not None:
                deps.append(lw)
        for k in w:
            lw = self.last_write.get(k)
            if lw is not None:
                deps.append(lw)
            deps.extend(self.readers.get(k, ()))
        seen = set()
        for d in deps:
            if d is op or id(d) in seen:
                continue
            seen.add(id(d))
            if d.dma_key is not None:
                op.dma_waits.append((d.dma_key, self.dcount[d.dma_key]))
            else:
                if d.eng == "pe" and eng == "pe":
                    continue
                d.signal = True
                op.deps.append(d)
        for k in r:
            self.readers.setdefault(k, []).append(op)
        for k in w:
            self.last_write[k] = op
            self.readers[k] = []
        if dma_key is not None:
            if dma_key not in self.dsem:
                self.dsem[dma_key] = self.stack.enter_context(self.nc.semaphore("d_" + str(dma_key)))
                self.dcount[dma_key] = 0
            self.dcount[dma_key] += 16
            op.dma_count = self.dcount[dma_key]
        self.ops.append(op)
        return op

    def mm(self, out, lhsT, rhs, start=True, stop=True, r=(), w=()):
        return self.add("pe", lambda e: e.matmul(out, lhsT, rhs, start=start, stop=stop), r, w)

    def tr(self, out, in_, ident, r=(), w=()):
        return self.add("pe", lambda e: e.transpose(out, in_, ident), r, w)

    def act(self, out, in_, func, r=(), w=(), bias=None, scale=None):
        kw = {}
        if bias is not None:
            kw["bias"] = bias
        if scale is not None:
            kw["scale"] = scale
        return self.add("act", lambda e: e.activation(out=out, in_=in_, func=func, **kw), r, w)

    def tt(self, eng, out, in0, in1, op, r=(), w=()):
        return self.add(eng, lambda e: e.tensor_tensor(out=out, in0=in0, in1=in1, op=op), r, w)

    def ts(self, eng, out, in0, s1, s2, op0, op1=None, r=(), w=()):
        if op1 is None:
            return self.add(eng, lambda e: e.tensor_scalar(out=out, in0=in0, scalar1=s1, scalar2=None, op0=op0), r, w)
        return self.add(eng, lambda e: e.tensor_scalar(out=out, in0=in0, scalar1=s1, scalar2=s2, op0=op0, op1=op1), r, w)

    def stt(self, eng, out, in0, scalar, in1, op0, op1, r=(), w=()):
        return self.add(eng, lambda e: e.scalar_tensor_tensor(out=out, in0=in0, scalar=scalar, in1=in1, op0=op0, op1=op1), r, w)

    def scan(self, eng, out, d0, d1, initial, r=(), w=()):
        return self.add(eng, lambda e: e.tensor_tensor_scan(out=out, data0=d0, data1=d1, initial=initial,
                                                            op0=ALU.mult, op1=ALU.add), r, w)

    def recip(self, out, in_, r=(), w=()):
        return self.add("dve", lambda e: e.reciprocal(out=out, in_=in_), r, w)

    def copy(self, eng, out, in_, r=(), w=()):
        if eng == "act":
            return self.add("act", lambda e: e.activation(out=out, in_=in_, func=AF.Identity), r, w)
        return self.add(eng, lambda e: e.tensor_copy(out=out, in_=in_), r, w)

    def memset(self, eng, ap, val, w=()):
        return self.add(eng, lambda e: e.memset(ap, val), (), w)

    def dma(self, q, out, in_, key, r=(), w=()):
        return self.add(q, lambda e: e.dma_start(out=out, in_=in_), r, w, dma_key=key)

    def flush(self, final=False):
        nc = self.nc
        ops = self.ops
        self.ops = []
        if not ops:
            return
        for op in ops:
            if op.dma_key is None and op.signal:
                self.ecount[op.eng] += 1
                op.count = self.ecount[op.eng]
        for op in ops:
            wd = self.waited[op.eng]
            need = {}
            for d in op.deps:
                s = self.esem[d.eng]
                if need.get(s.name, (None, 0))[1] < d.count:
                    need[s.name] = (s, d.count)
            for key, cnt in op.dma_waits:
                s = self.dsem[key]
                if need.get(s.name, (None, 0))[1] < cnt:
                    need[s.name] = (s, cnt)
            for name, (s, cnt) in need.items():
                if wd.get(name, 0) < cnt:
                    wd[name] = cnt
                    op.waits.append((s, cnt))
        by_eng = {e: [] for e in self.ENGS}
        for op in ops:
            by_eng[op.eng].append(op)
        finals = [(self.esem[e], self.ecount[e]) for e in self.esem if self.ecount[e] > 0]
        finals += [(self.dsem[k], self.dcount[k]) for k in self.dsem if final or not str(k).startswith("wc")]
        esem = self.esem
        dsem = self.dsem
        waited = self.waited

        def emit_stream(engname, eng):
            for op in by_eng[engname]:
                for s, cnt in op.waits:
                    eng.wait_ge(s, cnt)
                ins = op.fn(eng)
                if op.dma_key is not None:
                    ins.then_inc(dsem[op.dma_key], 16)
                elif op.signal:
                    ins.then_inc(esem[op.eng], 1)
            wd = waited[engname]
            for s, cnt in finals:
                if wd.get(s.name, 0) < cnt:
                    wd[s.name] = cnt
                    eng.wait_ge(s, cnt)

        with nc.Block() as block:
            @block.tensor
            def _(e):
                emit_stream("pe", e)

            @block.scalar
            def _(e):
                emit_stream("act", e)

            @block.vector
            def _(e):
                emit_stream("dve", e)

            @block.gpsimd
            def _(e):
                emit_stream("pool", e)

            @block.sync
            def _(e):
                emit_stream("sp", e)
        self.nblocks += 1
        self.last_write = {k: v for k, v in self.last_write.items()
                           if isinstance(k, tuple) and k[0] == "wb" and v.dma_key is not None}
        self.readers = {}


class Ring:
    def __init__(self, name, bufs):
        self.name = name
        self.bufs = bufs
        self.i = 0

    def next(self):
        j = self.i % len(self.bufs)
        self.i += 1
        return self.bufs[j], (self.name, j)


W_CH = 128 * 8192


def _ffn_layout(w_gu, w_dn):
    wg = w_gu[:, :DFF].reshape(8, 128, NOC_FF, 128)
    wu = w_gu[:, DFF:].reshape(8, 128, NOC_FF, 128)
    gu = np.stack([wg, wu], axis=3)
    gu = gu.transpose(2, 1, 0, 3, 4).reshape(NOC_FF // 2, 2, 128, 8, 256)
    gu = gu.transpose(0, 2, 1, 3, 4)
    dn = w_dn.reshape(NOC_FF, 128, 8, 128).transpose(2, 1, 0, 3)
    return np.ascontiguousarray(gu).ravel(), np.ascontiguousarray(dn).ravel()


def _sq_layout(w):
    a = w.reshape(8, 128, 8, 128).transpose(2, 1, 0, 3)
    a = a.reshape(2, 4, 128, 8, 128).transpose(0, 2, 1, 3, 4)
    return np.ascontiguousarray(a).ravel()


def host_weights(inp):
    f = np.float32
    parts = {}
    parts["w1gu"], parts["w1dn"] = _ffn_layout(inp["w_ffn1_gu"][0], inp["w_ffn1_down"][0])
    parts["w2gu"], parts["w2dn"] = _ffn_layout(inp["w_ffn2_gu"][0], inp["w_ffn2_down"][0])
    w_in = inp["w_in"][0]
    main_cols = np.concatenate([np.arange(0, 384), np.arange(416, 4512)])
    wm = w_in[:, main_cols]
    wm = np.concatenate([wm, np.zeros((1024, 128), f)], axis=1)
    wm = wm.reshape(8, 128, 36, 128).transpose(2, 1, 0, 3)
    wm = wm.reshape(9, 4, 128, 8, 128).transpose(0, 2, 1, 3, 4)
    parts["win"] = np.ascontiguousarray(wm).ravel()
    z64 = np.zeros((1024, 64), f)
    kr = w_in[:, 384:416]
    krs = np.concatenate([z64, kr[:, 16:32], kr[:, 0:16]], axis=1)
    kr = np.concatenate([z64, kr], axis=1)
    wkr = np.stack([kr, krs], axis=0).reshape(2, 8, 128, 96).transpose(2, 0, 1, 3)
    parts["wkr"] = np.ascontiguousarray(wkr).ravel()
    wqb = inp["w_q_b"][0].reshape(2, 128, NH, 96)
    wq = wqb.transpose(1, 0, 2, 3)
    wqs = np.concatenate([np.zeros_like(wqb[..., 0:64]), wqb[..., 80:96], wqb[..., 64:80]], axis=-1).transpose(1, 0, 2, 3)
    parts["wq"] = np.ascontiguousarray(wq).ravel()
    parts["wqs"] = np.ascontiguousarray(wqs).ravel()
    wkv = inp["w_kv_b"][0].reshape(128, NH, 128)
    wk = wkv[:, :, 0:64]
    wv = wkv[:, :, 64:128].reshape(128, 512)
    parts["wk"] = np.ascontiguousarray(wk).ravel()
    parts["wv"] = np.ascontiguousarray(wv).ravel()
    wao = inp["w_attn_o"][0].reshape(NH // 2, 128, 1024).transpose(1, 0, 2)
    parts["wao"] = np.ascontiguousarray(wao).ravel()
    wa = inp["lru_w_a"][0]
    wi = inp["lru_w_i"][0]
    wg = np.stack([wa, wi], axis=1)
    wg = wg.transpose(3, 0, 1, 2, 4)
    parts["wlg"] = np.ascontiguousarray(wg).ravel()
    parts["wlo"] = _sq_layout(inp["w_lru_o"][0])
    parts["wout"] = _sq_layout(inp["w_out"][0])
    offs = {}
    off = 0
    arrs = []
    order = ["wkr", "wq", "wqs", "wk", "wv", "w1gu", "w1dn", "win", "wlg", "wao", "wlo", "wout", "w2gu", "w2dn"]
    assert sorted(order) == sorted(parts)
    parts = {k: parts[k] for k in order}
    for k, v in parts.items():
        offs[k] = (off, v.size)
        arrs.append(v.astype(f, copy=False))
        off += v.size
    pad = (-off) % W_CH
    if pad:
        arrs.append(np.zeros(pad, f))
    wbig = np.concatenate(arrs)
    return wbig, offs


def host_small(inp):
    f = np.float32
    sm = {}
    wada = inp["w_ada"][0].reshape(8, 128, 72, 128).transpose(2, 1, 0, 3)
    wada = wada.reshape(36, 2, 128, 8, 128).transpose(0, 2, 1, 3, 4)
    sm["wada"] = np.ascontiguousarray(wada)
    sm["bada"] = np.ascontiguousarray(inp["b_ada"][0].reshape(72, 128).T)
    sm["gpre"] = np.ascontiguousarray(inp["g_pre"][0].reshape(3, 8, 128).transpose(2, 0, 1))
    sm["gpost"] = np.ascontiguousarray(inp["g_post"][0].reshape(3, 8, 128).transpose(2, 0, 1))
    sm["gq"] = np.ascontiguousarray(inp["g_q_norm"][0].reshape(2, 128).T)
    sm["gkv"] = np.ascontiguousarray(inp["g_kv_norm"][0].reshape(1, 128).T)
    lb = np.stack([inp["lru_b_a"][0], inp["lru_b_i"][0]], axis=1)
    sm["lrub"] = np.ascontiguousarray(lb.reshape(2, 2, 8, 128).transpose(3, 0, 1, 2))
    sm["lam"] = np.ascontiguousarray(inp["lru_lambda"][0].reshape(2, 8, 128).transpose(2, 0, 1))
    sm["cw"] = np.ascontiguousarray(inp["conv_w"][0].reshape(4, 8, 128).transpose(2, 0, 1))
    sm["cb"] = np.ascontiguousarray(inp["conv_b"][0].reshape(8, 128).T)
    sm["ident"] = np.eye(128, dtype=f)
    smax = 4096
    inv = (1.0 / (np.float32(10000.0) ** (np.arange(0, 32, 2, dtype=f) / np.float32(32)))).astype(f)
    ang = (np.arange(smax, dtype=f)[:, None] * inv[None, :]).astype(f)
    cos = np.cos(ang).astype(f).T
    sin = np.sin(ang).astype(f).T
    sm["ropec"] = np.ascontiguousarray(np.concatenate([cos, cos], axis=0))
    sm["ropes"] = np.ascontiguousarray(np.concatenate([-sin, sin], axis=0))
    return sm


def build(seq_lens, wsize, woffs, debug=False, phases="0ALBC"):
    nc = bass.Bass("TRN2", target_bir_lowering=False)
    NTOK = sum(seq_lens)
    NT = NTOK // T
    tiles = []
    for s, S in enumerate(seq_lens):
        for ti in range(S // T):
            tiles.append((s, ti, ti * T, len(tiles)))
    NS = len(seq_lens)
    scr_kind = "ExternalOutput" if debug else "Internal"

    def din(name, shape, dt=F32):
        return nc.dram_tensor(name, list(shape), dt, kind="ExternalInput").ap()

    def dscr(name, shape, dt):
        return nc.dram_tensor(name, list(shape), dt, kind=scr_kind).ap()

    x_d = din("x", [NTOK, D])
    y_d = nc.dram_tensor("y", [NTOK, D], F32, kind="ExternalOutput").ap()
    cT_d = din("cT", [128, 8, NS])
    wbig_d = din("wbig", [wsize // W_CH, 128, W_CH // 128])
    wada_d = din("wada", [36, 128, 2, 8, 128])
    bada_d = din("bada", [128, 72])
    gpre_d = din("gpre", [128, 3, 8])
    gpost_d = din("gpost", [128, 3, 8])
    gq_d = din("gq", [128, 2])
    gkv_d = din("gkv", [128, 1])
    lrub_d = din("lrub", [128, 2, 2, 8])
    lam_d = din("lam", [128, 2, 8])
    cw_d = din("cw", [128, 4, 8])
    cb_d = din("cb", [128, 8])
    ident_d = din("ident", [128, 128])
    ropec_d = din("ropec", [32, 4096])
    ropes_d = din("ropes", [32, 4096])

    wb_d = nc.dram_tensor("wb", [wsize // W_CH, 128, W_CH // 128], BF16, kind="Internal").ap()
    wb_flat = wb_d.rearrange("a p n -> (a p n)")

    class WView:
        def __init__(self, name, G=None):
            self.off, self.n = woffs[name]
            self.G = G
            flat = wb_flat[self.off:self.off + self.n]
            if G is None:
                self.ap = flat.rearrange("(p n) -> p n", p=128)
            else:
                self.ap = flat.rearrange("(g p n) -> g p n", g=G, p=128)

        def __getitem__(self, g):
            return self.ap[g]

        def keys(self, g=None):
            if g is None:
                lo, hi = self.off, self.off + self.n
            else:
                sz = self.n // self.G
                lo, hi = self.off + g * sz, self.off + (g + 1) * sz
            return [("wb", a) for a in range(lo // W_CH, (hi - 1) // W_CH + 1)]

    w1gu_v = WView("w1gu", 11)
    w1dn_v = WView("w1dn", 8)
    w2gu_v = WView("w2gu", 11)
    w2dn_v = WView("w2dn", 8)
    win_v = WView("win", 9)
    wkr_v = WView("wkr")
    wq_v = WView("wq")
    wqs_v = WView("wqs")
    wk_v = WView("wk")
    wv_v = WView("wv")
    wao_v = WView("wao")
    wlg_v = WView("wlg")
    wlo_v = WView("wlo", 2)
    wout_v = WView("wout", 2)

    X1_d = dscr("X1", [NT, 128, 8 * T], F32)
    QT_d = dscr("QT", [NT, 96, 8 * T], BF16)
    KT_d = dscr("KT", [NT, 96, 8 * T], BF16)
    V_d = dscr("V", [NT, 128, 4 * 512], BF16)
    XL_d = [dscr("XL%d" % s, [128, 8, S], BF16) for s, S in enumerate(seq_lens)]
    GY_d = dscr("GY", [NT, 128, 8 * T], BF16)
    SA_d = dscr("SA", [NT, 128, 8 * T], BF16)
    SL_d = dscr("SL", [NT, 128, 8 * T], BF16)
    HF_d = dscr("HF", [NT, 128, 8 * T], F32)
    XC_d = dscr("XC", [NT, 128, 8 * T], F32)
    XCB_d = dscr("XCB", [NT, 128, 8 * T], BF16)
    HY_d = dscr("HY", [NT, 128, 8 * T], BF16)
    OT_d = dscr("OT", [NT, 128, 4 * T], BF16)
    MOD_d = dscr("MODP", [128, NS * 3 * 3 * 8], F32) if debug else None

    stack = contextlib.ExitStack()
    with stack:
        stack.enter_context(nc.allow_low_precision("bf16 matmul operands, fp32 accumulation"))
        stack.enter_context(nc.allow_non_contiguous_dma("strided layouts"))
        P = Prog(nc, stack)

        def sb(name, shape, dt, st=None):
            return (st or stack).enter_context(nc.sbuf_tensor("sb_" + name, list(shape), dt))

        pp = [stack.enter_context(nc.psum_tensor("pp%d" % i, [128, 1024], F32)) for i in range(4)]
        ps = [pp[i // 2][:, (i % 2) * 512:(i % 2 + 1) * 512] for i in range(8)]

        def pk(i):
            return ("ps", i)

        ident = sb("ident", [128, 128], F32)
        ones_b = sb("ones_b", [128, 128], BF16)
        ones_f = sb("ones_f", [128, 128], F32)
        modp = sb("modp", [128, NS, 3, 3, 8], F32)
        gq = sb("gq", [128, 2], F32)
        gkv = sb("gkv", [128, 1], F32)
        lrub = sb("lrub", [128, 2, 2, 8], F32)
        nsp = sb("nsp", [128, 2, 8], F32)
        nsp2 = sb("nsp2", [128, 2, 8], F32)
        lrubh = sb("lrubh", [128, 2, 2, 8], F32)
        quart = sb("quart", [128, 1], F32)
        cb = sb("cb", [128, 8], F32)
        cw = sb("cw", [128, 4, 8], F32)
        eps_t = sb("eps_t", [128, 1], F32)

        with contextlib.ExitStack() as st0:
            bada = sb("bada", [128, 72], F32, st0)
            gpre = sb("gpre", [128, 3, 8], F32, st0)
            gpost = sb("gpost", [128, 3, 8], F32, st0)
            lam = sb("lam", [128, 2, 8], F32, st0)
            cT = sb("cT", [128, 8, NS], F32, st0)
            sc = sb("sc", [128, 8, NS], F32, st0)
            mod = sb("mod", [128, 72, NS], F32, st0)
            tmp8 = sb("tmp8", [128, 16], F32, st0)
            aring = Ring("aring", [sb("aring%d" % i, [128, 2, 8, 128], F32, st0) for i in range(3)])

            for name, dst, src in (("ident", ident, ident_d), ("bada", bada, bada_d), ("gpre", gpre, gpre_d),
                                   ("gpost", gpost, gpost_d), ("gq", gq, gq_d), ("gkv", gkv, gkv_d),
                                   ("lrub", lrub, lrub_d), ("lam", lam, lam_d), ("cw", cw, cw_d),
                                   ("cb", cb, cb_d), ("cT", cT, cT_d)):
                P.dma("sp", dst[:], src, "c_" + name, w=[name])
            n_first = min(wsize // W_CH, (w1gu_v.off + w1gu_v.n + W_CH - 1) // W_CH)
            for a in range(n_first):
                P.dma("pool", wb_d[a], wbig_d[a], "wc%d" % a, w=[("wb", a)])
            P.memset("dve", ones_b[:], 1.0, w=["ones_b"])
            P.memset("dve", ones_f[:], 1.0, w=["ones_f"])
            P.memset("dve", eps_t[:], EPS, w=["eps_t"])
            P.act(sc[:], cT[:], AF.Silu, r=["cT"], w=["sc"])
            for grp in range(36):
                slot, skey = aring.next()
                P.dma("sp", slot[:], wada_d[grp], skey, w=[skey])
                for j in range(2):
                    oc = grp * 2 + j
                    for kc in range(8):
                        P.mm(ps[0][:, oc * NS:(oc + 1) * NS], slot[:, j, kc, :], sc[:, kc, :],
                             start=(kc == 0), stop=(kc == 7), r=[skey, "sc"], w=[pk(0)])
            for a in range(n_first, wsize // W_CH):
                P.dma("pool", wb_d[a], wbig_d[a], "wc%d" % a, r=[("aring", 0), ("aring", 1), ("aring", 2)],
                      w=[("wb", a)])
            psv = ps[0][:, 0:72 * NS].rearrange("p (o s) -> p o s", s=NS)
            for s in range(NS):
                P.tt("dve", mod[:, :, s], psv[:, :, s], bada[:], ALU.add, r=[pk(0), "bada"], w=["mod"])
            for s in range(NS):
                for j in range(3):
                    coef = 1.0 if j == 1 else 0.5
                    o = j * 3 * 8
                    P.ts("dve", tmp8[:, 0:8], mod[:, o + 8:o + 16, s], 1.0, None, ALU.add, r=["mod"], w=["tmp8a"])
                    P.tt("dve", modp[:, s, j, 0, :], tmp8[:, 0:8], gpre[:, j, :], ALU.mult,
                         r=["tmp8a", "gpre"], w=["modp"])
                    P.copy("dve", modp[:, s, j, 1, :], mod[:, o:o + 8, s], r=["mod"], w=["modp"])
                    P.ts("dve", tmp8[:, 8:16], mod[:, o + 16:o + 24, s], coef, None, ALU.mult, r=["mod"], w=["tmp8b"])
                    P.tt("dve", modp[:, s, j, 2, :], tmp8[:, 8:16], gpost[:, j, :], ALU.mult,
                         r=["tmp8b", "gpost"], w=["modp"])
            P.act(lam[:], lam[:], AF.Exp, r=["lam"], w=["lam"], scale=-1.0)
            P.act(lam[:], lam[:], AF.Ln, r=["lam", "ones_f"], w=["lam"], bias=ones_f[:, 0:1])
            P.ts("dve", nsp[:], lam[:], -8.0, None, ALU.mult, r=["lam"], w=["nsp"])
            P.ts("dve", nsp2[:], lam[:], -4.0, None, ALU.mult, r=["lam"], w=["nsp"])
            P.ts("dve", lrubh[:].rearrange("p a b c -> p (a b c)"), lrub[:].rearrange("p a b c -> p (a b c)"), 0.5, None,
                 ALU.mult, r=["lrub"], w=["lrubh"])
            P.memset("dve", quart[:], 0.25, w=["quart"])
            if debug:
                P.dma("pool", MOD_d, modp[:].rearrange("p a b c d -> p (a b c d)"), "dbg", r=["modp"])
            P.flush()
            if "A" not in phases:
                return nc

        def rstd_from_sq(sq_chunks, n, dim, rstd, keys_r):
            for i in range(n):
                P.mm(ps[6][:, :], ones_b[:, :], sq_chunks(i), start=(i == 0), stop=(i == n - 1),
                     r=keys_r + ["ones_b"], w=[pk(6)])
            P.act(rstd[:], ps[6][:, :], AF.Sqrt, r=[pk(6), "eps_t"], w=["rstd"], scale=1.0 / dim, bias=eps_t[:, 0:1])
            P.recip(rstd[:], rstd[:], r=["rstd"], w=["rstd"])

        def pre_norm(xT, sq, hT, rstd, tmpf, s, j):
            rstd_from_sq(lambda i: sq[:, i, :], 8, D, rstd, [("sq", i) for i in range(8)])
            for fc in range(8):
                tb = tmpf[fc % 2]
                tk = ("tmpf", fc % 2)
                P.tt("dve", tb[:], xT[:, fc, :], rstd[:], ALU.mult, r=[("xT", fc), "rstd"], w=[tk])
                P.act(hT[:, fc, :], tb[:], AF.Identity, r=[tk, "modp"], w=[("hT", fc)],
                      scale=modp[:, s, j, 0, fc:fc + 1], bias=modp[:, s, j, 1, fc:fc + 1])

        def post_norm_residual(xT, fT, sq, rstd, tmpf, s, j):
            rstd_from_sq(lambda i: sq[:, i, :], 8, D, rstd, [("sq", i) for i in range(8)])
            for fc in range(8):
                tb = tmpf[fc % 2]
                tk = ("tmpf", fc % 2)
                P.tt("dve", tb[:], fT[:, fc, :], rstd[:], ALU.mult, r=[("fT", fc), "rstd"], w=[tk])
                P.stt("dve", xT[:, fc, :], tb[:], modp[:, s, j, 2, fc:fc + 1], xT[:, fc, :], ALU.mult, ALU.add,
                      r=[tk, "modp", ("xT", fc)], w=[("xT", fc)])

        def ffn(hT, actT, fT, sq, silu_t, wring, wgu_v, wdn_v):
            for _ in gen_ffn_up(hT, actT, silu_t, wring, wgu_v):
                pass
            for _ in gen_ffn_down(actT, fT, sq, wring, wdn_v):
                pass

        def gen_ffn_up(hT, actT, silu_t, wring, wgu_v):
            for grp in range(11):
                slot, skey = wring.next()
                P.dma("sp", slot[:, :], wgu_v[grp], skey, r=wgu_v.keys(grp), w=[skey])
                sv = slot[:, :].rearrange("p (j k n) -> p j k n", j=2, k=8)
                for j in range(2):
                    oc = grp * 2 + j
                    bg, bu = (0, 1) if oc % 2 == 0 else (2, 3)
                    hk = [("hT", k) for k in range(8)]
                    for kc in range(8):
                        P.mm(ps[bg][:, :], sv[:, j, kc, 0:128], hT[:, kc, :], start=(kc == 0), stop=(kc == 7),
                             r=[skey] + hk, w=[pk(bg)])
                    for kc in range(8):
                        P.mm(ps[bu][:, :], sv[:, j, kc, 128:256], hT[:, kc, :], start=(kc == 0), stop=(kc == 7),
                             r=[skey] + hk, w=[pk(bu)])
                    stb = silu_t[oc % 2]
                    sk = ("silu", oc % 2)
                    P.act(stb[:], ps[bg][:, :], AF.Silu, r=[pk(bg)], w=[sk])
                    P.tt("dve", actT[:, oc, :], ps[bu][:, :], stb[:], ALU.mult, r=[pk(bu), sk], w=[("actT", oc)])
                    yield

        def gen_ffn_down(actT, fT, sq, wring, wdn_v, fkey="fT", sqkey="sq"):
            ak = [("actT", k) for k in range(NOC_FF)]
            for oc in range(8):
                slot, skey = wring.next()
                P.dma("sp", slot[:, 0:NOC_FF * 128], wdn_v[oc], skey, r=wdn_v.keys(oc), w=[skey])
                sv = slot[:, 0:NOC_FF * 128].rearrange("p (k n) -> p k n", k=NOC_FF)
                b = 4 + oc % 2
                for kc in range(NOC_FF):
                    P.mm(ps[b][:, :], sv[:, kc, :], actT[:, kc, :], start=(kc == 0), stop=(kc == NOC_FF - 1),
                         r=[skey] + ak, w=[pk(b)])
                P.copy("dve", fT[:, oc, :], ps[b][:, :], r=[pk(b)], w=[(fkey, oc)])
                P.act(sq[:, oc, :], fT[:, oc, :], AF.Square, r=[(fkey, oc)], w=[(sqkey, oc)])
                yield

        with contextlib.ExitStack() as sa:
            xin = sb("xin", [128, 4, D], F32, sa)
            xT = sb("xT", [128, 8, T], F32, sa)
            sq = sb("sq", [128, 8, T], BF16, sa)
            hT = sb("hT", [128, 8, T], BF16, sa)
            actT = sb("actT", [128, NOC_FF, T], BF16, sa)
            fT = sb("fT", [128, 8, T], F32, sa)
            silu_t = [sb("silu%d" % i, [128, T], F32, sa) for i in range(2)]
            tmpf = [sb("tmpf%d" % i, [128, T], F32, sa) for i in range(2)]
            rstd = sb("rstd", [128, T], F32, sa)
            cq = sb("cq", [128, 2, T], F32, sa)
            sqq = sb("sqq", [128, 2, T], BF16, sa)
            cqn = sb("cqn", [128, 2, T], BF16, sa)
            ckv = sb("ckv", [128, T], F32, sa)
            ckvn = sb("ckvn", [128, T], BF16, sa)
            stg = Ring("stg", [sb("stg%d" % i, [128, 4 * T], BF16, sa) for i in range(3)])
            stq = Ring("stq", [sb("stq%d" % i, [128, 4 * T], BF16, sa) for i in range(3)])
            wring = Ring("wring", [sb("wring%d" % i, [128, 4096], BF16, sa) for i in range(4)])
            wq_s = sb("wq_s", [128, 2, NH, 96], BF16, sa)
            wqs_s = sb("wqs_s", [128, 2, NH, 96], BF16, sa)
            wk_s = sb("wk_s", [128, NH, 64], BF16, sa)
            wv_s = sb("wv_s", [128, 512], BF16, sa)
            wkr_s = sb("wkr_s", [128, 2, 8, 96], BF16, sa)
            ropec = sb("ropec", [96, T], F32, sa)
            ropes = sb("ropes", [96, T], F32, sa)
            rt1 = sb("rt1", [96, T], F32, sa)
            rt2 = sb("rt2", [96, T], F32, sa)

            P.dma("sp", wq_s[:].rearrange("p a h c -> p (a h c)"), wq_v.ap, "c_wq", r=wq_v.keys(), w=["wq"])
            P.dma("sp", wqs_s[:].rearrange("p a h c -> p (a h c)"), wqs_v.ap, "c_wqs", r=wqs_v.keys(), w=["wqs"])
            P.dma("sp", wk_s[:].rearrange("p h c -> p (h c)"), wk_v.ap, "c_wk", r=wk_v.keys(), w=["wk"])
            P.dma("sp", wv_s[:], wv_v.ap, "c_wv", r=wv_v.keys(), w=["wv"])
            P.dma("sp", wkr_s[:].rearrange("p a k c -> p (a k c)"), wkr_v.ap, "c_wkr", r=wkr_v.keys(), w=["wkr"])

            hTb = sb("hTb", [128, 8, T], BF16, sa)
            sqx = sb("sqx", [128, 8, T], BF16, sa)

            def gen_N0(tile):
                s, ti, t0, g = tile
                tok0 = g * T
                P.dma("sp", xin[:], x_d[tok0:tok0 + T, :].rearrange("(a p) d -> p a d", p=128), "xin", w=["xin"])
                for fc in range(8):
                    for a in range(4):
                        P.tr(ps[7][:, a * 128:(a + 1) * 128], xin[:, a, fc * 128:(fc + 1) * 128], ident[:],
                             r=["xin", "ident"], w=[pk(7)])
                    P.copy("dve", xT[:, fc, :], ps[7][:, :], r=[pk(7)], w=[("xT", fc)])
                    P.act(sqx[:, fc, :], xT[:, fc, :], AF.Square, r=[("xT", fc)], w=[("sqx", fc)])
                    if fc % 2 == 1:
                        yield
                rstd_from_sq(lambda i: sqx[:, i, :], 8, D, rstd, [("sqx", i) for i in range(8)])
                yield
                for fc in range(8):
                    tb = tmpf[fc % 2]
                    tk = ("tmpf", fc % 2)
                    P.tt("dve", tb[:], xT[:, fc, :], rstd[:], ALU.mult, r=[("xT", fc), "rstd"], w=[tk])
                    P.act(hT[:, fc, :], tb[:], AF.Identity, r=[tk, "modp"], w=[("hT", fc)],
                          scale=modp[:, s, 0, 0, fc:fc + 1], bias=modp[:, s, 0, 1, fc:fc + 1])
                    if fc % 4 == 3:
                        yield

            def gen_N1(tile):
                s, ti, t0, g = tile
                tok0 = g * T
                rstd_from_sq(lambda i: sq[:, i, :], 8, D, rstd, [("sq", i) for i in range(8)])
                P.dma("sp", xin[:], x_d[tok0:tok0 + T, :].rearrange("(a p) d -> p a d", p=128), "xin", w=["xin"])
                yield
                for fc in range(8):
                    for a in range(4):
                        P.tr(ps[7][:, a * 128:(a + 1) * 128], xin[:, a, fc * 128:(fc + 1) * 128], ident[:],
                             r=["xin", "ident"], w=[pk(7)])
                    P.copy("dve", xT[:, fc, :], ps[7][:, :], r=[pk(7)], w=[("xT", fc)])
                    tb = tmpf[fc % 2]
                    tk = ("tmpf", fc % 2)
                    P.tt("dve", tb[:], fT[:, fc, :], rstd[:], ALU.mult, r=[("fT", fc), "rstd"], w=[tk])
                    P.stt("dve", xT[:, fc, :], tb[:], modp[:, s, 0, 2, fc:fc + 1], xT[:, fc, :], ALU.mult, ALU.add,
                          r=[tk, "modp", ("xT", fc)], w=[("xT", fc)])
                    P.act(sq[:, fc, :], xT[:, fc, :], AF.Square, r=[("xT", fc)], w=[("sq", fc)])
                    yield
                P.dma("pool", X1_d[g], xT[:].rearrange("p c t -> p (c t)"), "st_x1", r=[("xT", k) for k in range(8)],
                      w=[("X1", g)])
                rstd_from_sq(lambda i: sq[:, i, :], 8, D, rstd, [("sq", i) for i in range(8)])
                yield
                for fc in range(8):
                    tb = tmpf[fc % 2]
                    tk = ("tmpf", fc % 2)
                    P.tt("dve", tb[:], xT[:, fc, :], rstd[:], ALU.mult, r=[("xT", fc), "rstd"], w=[tk])
                    P.act(hTb[:, fc, :], tb[:], AF.Identity, r=[tk, "modp"], w=[("hTb", fc)],
                          scale=modp[:, s, 1, 0, fc:fc + 1], bias=modp[:, s, 1, 1, fc:fc + 1])
                    if fc % 2 == 1:
                        yield

            def stage_W(tile):
                s, ti, t0, g = tile
                P.dma("sp", ropec[64:96, :], ropec_d[:, t0:t0 + T], "rope", w=["ropec"])
                P.dma("sp", ropes[64:96, :], ropes_d[:, t0:t0 + T], "rope", w=["ropes"])
                hk = [("hTb", k) for k in range(8)]
                units = []

                def u_lat():
                    rstd_from_sq(lambda i: sqq[:, i, :], 2, 256.0, rstd, [("sqq", 0), ("sqq", 1)])
                    for i in range(2):
                        P.stt("dve", cqn[:, i, :], cq[:, i, :], gq[:, i:i + 1], rstd[:], ALU.mult, ALU.mult,
                              r=[("cq", i), "gq", "rstd"], w=["cqn"])
                    P.act(sqq[:, 0, :], ckv[:], AF.Square, r=["ckv"], w=[("sqq", 0)])
                    rstd_from_sq(lambda i: sqq[:, 0, :], 1, 128.0, rstd, [("sqq", 0)])
                    P.stt("dve", ckvn[:], ckv[:], gkv[:, 0:1], rstd[:], ALU.mult, ALU.mult,
                          r=["ckv", "gkv", "rstd"], w=["ckvn"])
                units.append(u_lat)
                qst = {}

                def u_q(h):
                    hh_, hl = h // 4, h % 4
                    if hl == 0:
                        qst[hh_] = stq.next()
                    buf, key = qst[hh_]
                    qv = buf[0:96, :].rearrange("p (h t) -> p h t", h=4)
                    b = 4 + h % 2
                    for kc in range(2):
                        P.mm(ps[b][0:96, :], wq_s[:, kc, h, :], cqn[:, kc, :], start=(kc == 0), stop=(kc == 1),
                             r=["wq", "cqn"], w=[pk(b)])
                    for kc in range(2):
                        P.mm(ps[7][0:96, :], wqs_s[:, kc, h, :], cqn[:, kc, :], start=(kc == 0), stop=(kc == 1),
                             r=["wqs", "cqn"], w=[pk(7)])
                    P.copy("act", qv[0:64, hl, :], ps[b][0:64, :], r=[pk(b)], w=[(key, hl, "n")])
                    P.tt("dve", rt1[64:96, :], ps[b][64:96, :], ropec[64:96, :], ALU.mult, r=[pk(b), "ropec"], w=["rt1"])
                    P.tt("dve", rt2[64:96, :], ps[7][64:96, :], ropes[64:96, :], ALU.mult, r=[pk(7), "ropes"], w=["rt2"])
                    P.tt("dve", qv[64:96, hl, :], rt1[64:96, :], rt2[64:96, :], ALU.add, r=["rt1", "rt2"], w=[(key, hl, "r")])
                    if hl == 3:
                        P.dma("pool", QT_d[g][:, hh_ * 4 * T:(hh_ + 1) * 4 * T], buf[0:96, :], "st_q",
                              r=[(key, k, x) for k in range(4) for x in "nr"], w=[("QT", g, hh_)])
                for h in range(NH):
                    units.append(lambda h=h: u_q(h))

                def u_kr():
                    for kc in range(8):
                        P.mm(ps[4][0:96, :], wkr_s[:, 0, kc, :], hTb[:, kc, :], start=(kc == 0), stop=(kc == 7),
                             r=["wkr"] + hk, w=[pk(4)])
                    for kc in range(8):
                        P.mm(ps[5][0:96, :], wkr_s[:, 1, kc, :], hTb[:, kc, :], start=(kc == 0), stop=(kc == 7),
                             r=["wkr"] + hk, w=[pk(5)])
                    P.tt("dve", rt1[64:96, :], ps[4][64:96, :], ropec[64:96, :], ALU.mult, r=[pk(4), "ropec"], w=["rt1"])
                    P.tt("dve", rt2[64:96, :], ps[5][64:96, :], ropes[64:96, :], ALU.mult, r=[pk(5), "ropes"], w=["rt2"])
                    P.tt("dve", rt1[64:96, :], rt1[64:96, :], rt2[64:96, :], ALU.add, r=["rt1", "rt2"], w=["rt1"])
                units.append(u_kr)

                def u_k(hh_):
                    buf, key = stq.next()
                    kv = buf[0:96, :].rearrange("p (h t) -> p h t", h=4)
                    for hl in range(4):
                        h = hh_ * 4 + hl
                        P.copy("pool", kv[64:96, hl, :], rt1[64:96, :], r=["rt1"], w=[(key, hl, "r")])
                        b = 4 + h % 2
                        P.mm(ps[b][0:64, :], wk_s[:, h, :], ckvn[:], r=["wk", "ckvn"], w=[pk(b)])
                        P.copy("act", kv[0:64, hl, :], ps[b][0:64, :], r=[pk(b)], w=[(key, hl, "n")])
                    P.dma("pool", KT_d[g][:, hh_ * 4 * T:(hh_ + 1) * 4 * T], buf[0:96, :], "st_k",
                          r=[(key, k, x) for k in range(4) for x in "nr"], w=[("KT", g, hh_)])
                units.append(lambda: u_k(0))
                units.append(lambda: u_k(1))

                def u_v():
                    buf, key = stq.next()
                    for a in range(4):
                        b = 4 + a % 2
                        P.mm(ps[b][:, :], ckvn[:, a * 128:(a + 1) * 128], wv_s[:], r=["wv", "ckvn"], w=[pk(b)])
                        P.copy("act", buf[:, a * 512:(a + 1) * 512], ps[b][:, :], r=[pk(b)],
                               w=[(key, a, "n"), (key, a, "r")])
                    P.dma("pool", V_d[g], buf[:, 0:2048], "st_v", r=[(key, a, x) for a in range(4) for x in "nr"],
                          w=[("V", g)])
                units.append(u_v)

                cur = {}
                for grp in range(9):
                    slot, skey = wring.next()
                    P.dma("sp", slot[:, :], win_v[grp], skey, r=win_v.keys(grp), w=[skey])
                    sv = slot[:, :].rearrange("p (j k n) -> p j k n", j=4, k=8)
                    for j in range(4):
                        oc = grp * 4 + j
                        if oc >= 35:
                            continue
                        b = oc % 4
                        for kc in range(8):
                            P.mm(ps[b][:, :], sv[:, j, kc, :], hTb[:, kc, :], start=(kc == 0), stop=(kc == 7),
                                 r=[skey] + hk, w=[pk(b)])
                        if oc < 2:
                            P.copy("dve", cq[:, oc, :], ps[b][:, :], r=[pk(b)], w=[("cq", oc)])
                            P.act(sqq[:, oc, :], cq[:, oc, :], AF.Square, r=[("cq", oc)], w=[("sqq", oc)])
                        elif oc == 2:
                            P.copy("dve", ckv[:], ps[b][:, :], r=[pk(b)], w=["ckv"])
                        else:
                            c = (oc - 3) % 8
                            kind = ("xl", "gy", "sa", "sl")[(oc - 3) // 8]
                            cl = c % 4
                            if cl == 0:
                                cur[kind] = stg.next()
                            st_, skey2 = cur[kind]
                            dst = st_[:, cl * T:(cl + 1) * T]
                            if kind == "xl":
                                P.copy("dve", dst, ps[b][:, :], r=[pk(b)], w=[(skey2, cl)])
                            elif kind == "gy":
                                P.act(dst, ps[b][:, :], AF.Gelu, r=[pk(b)], w=[(skey2, cl)])
                            else:
                                P.act(dst, ps[b][:, :], AF.Sigmoid, r=[pk(b)], w=[(skey2, cl)])
                            if cl == 3:
                                c0 = c - 3
                                rk = [(skey2, k) for k in range(4)]
                                if kind == "xl":
                                    P.dma("pool", XL_d[s][:, c0:c0 + 4, t0:t0 + T],
                                          st_[:, :].rearrange("p (c t) -> p c t", c=4), "st_xl", r=rk, w=[("XL", s)])
                                else:
                                    dd = {"gy": GY_d, "sa": SA_d, "sl": SL_d}[kind]
                                    P.dma("pool", dd[g][:, c0 * T:(c0 + 4) * T], st_[:, :], "st_" + kind, r=rk,
                                          w=[(kind, g, c0)])
                            if oc >= 3 and (oc - 3) % 2 == 1 and units:
                                units.pop(0)()
                while units:
                    units.pop(0)()

            def run(gen):
                for _ in gen:
                    pass

            def interleave(main, side, every=1, start=0):
                n = 0
                live = side is not None
                for _ in main:
                    n += 1
                    if live and n > start and (n - start) % every == 0:
                        try:
                            next(side)
                        except StopIteration:
                            live = False
                if live:
                    run(side)

            ntl = len(tiles)
            run(gen_N0(tiles[0]))
            for i in range(ntl):
                interleave(gen_ffn_up(hT, actT, silu_t, wring, w1gu_v), gen_N1(tiles[i - 1]) if i > 0 else None)
                if i > 0:
                    stage_W(tiles[i - 1])
                interleave(gen_ffn_down(actT, fT, sq, wring, w1dn_v), gen_N0(tiles[i + 1]) if i + 1 < ntl else None)
            run(gen_N1(tiles[ntl - 1]))
            stage_W(tiles[ntl - 1])
            P.flush()

            if "L" not in phases:
                return nc

        def rev(ap2):
            n = ap2.shape[-1]
            last = ap2[:, n - 1:n]
            return AP(last.tensor, last.offset, [list(last.ap[0]), [-1, n]])

        with contextlib.ExitStack() as sl:
            xlr = Ring("xlh", [sb("xlh%d" % i, [128, 8, T + 3], BF16, sl) for i in range(2)])
            xc = sb("xc", [128, 8, T], F32, sl)
            xcb = sb("xcb", [128, 8, T], BF16, sl)
            rr = sb("rr", [128, 8, T], F32, sl)
            ii = sb("ii", [128, 8, T], F32, sl)
            a2 = sb("a2", [128, 8, T], F32, sl)
            hh = sb("hh", [128, 8, T], F32, sl)
            hfl = sb("hfl", [128, 8, T], F32, sl)
            gyl = sb("gyl", [128, 8, T], BF16, sl)
            hy = sb("hy", [128, 8, T], BF16, sl)
            state = sb("state", [128, 8], F32, sl)
            wlg_s = sb("wlg_s", [128, 2, 2, 8, 128], BF16, sl)
            diag = sb("diag", [128, 4, 8, 128], BF16, sl)
            for j in range(4):
                for c in range(8):
                    P.ts("dve", diag[:, j, c, :], ident[:], cw[:, j, c:c + 1], None, ALU.mult,
                         r=["ident", "cw"], w=["diag"])
            P.dma("sp", wlg_s[:].rearrange("p a b c e -> p (a b c e)"), wlg_v.ap, "c_wlg", r=wlg_v.keys(), w=["wlg"])
            for s, S in enumerate(seq_lens):
                stiles = [tl for tl in tiles if tl[0] == s]
                for d in (0, 1):
                    order = stiles if d == 0 else stiles[::-1]
                    P.memset("dve", state[:], 0.0, w=[("state", c) for c in range(8)])
                    tinfo = {}

                    def tile_loads(ti, t0, g, S=S, s=s, d=d, tinfo=tinfo):
                        xb, xkey = xlr.next()
                        lo = max(t0 - 1, 0)
                        hi = min(t0 + T + 2, S)
                        d0 = lo - (t0 - 1)
                        P.dma("sp", xb[:, :, d0:d0 + (hi - lo)], XL_d[s][:, :, lo:hi], xkey, w=[xkey])
                        if t0 == 0:
                            P.memset("pool", xb[:, :, 0:1], 0.0, w=[xkey])
                        if t0 + T == S:
                            P.memset("pool", xb[:, :, T + 1:T + 3], 0.0, w=[xkey])
                        tinfo[ti] = (xb, xkey)

                    def stage_a(ti, t0, g, half, d=d, tinfo=tinfo):
                        cs = list(range(4 * half, 4 * half + 4))
                        if d == 0:
                            if half == 0:
                                tile_loads(ti, t0, g)
                            xb, xkey = tinfo[ti]
                        else:
                            c0 = 4 * half
                            P.dma("sp", xc[:, c0:c0 + 4, :].rearrange("p c t -> p (c t)"), XC_d[g][:, c0 * T:(c0 + 4) * T],
                                  "ld_xc%d" % half, r=[("XC", g, half)], w=[("xc", c) for c in cs])
                            P.dma("sp", xcb[:, c0:c0 + 4, :].rearrange("p c t -> p (c t)"), XCB_d[g][:, c0 * T:(c0 + 4) * T],
                                  "ld_xcb%d" % half, r=[("XCB", g, half)], w=[("xcb", c) for c in cs])

                        def conv(c):
                            b = (0, 1, 6, 7)[c % 4]
                            for j in range(4):
                                P.mm(ps[b][:, :], diag[:, j, c, :], xb[:, c, j:j + T], start=(j == 0), stop=(j == 3),
                                     r=[xkey, "diag"], w=[pk(b)])
                            P.ts("dve", xc[:, c, :], ps[b][:, :], cb[:, c:c + 1], None, ALU.add,
                                 r=[pk(b), "cb"], w=[("xc", c)])
                            P.copy("dve", xcb[:, c, :], xc[:, c, :], r=[("xc", c)], w=[("xcb", c)])

                        def gates(c):
                            b1, b2 = (2, 3) if c % 2 == 0 else (4, 5)
                            P.mm(ps[b1][:, :], wlg_s[:, d, 0, c, :], xcb[:, c, :], r=["wlg", ("xcb", c)], w=[pk(b1)])
                            P.mm(ps[b2][:, :], wlg_s[:, d, 1, c, :], xcb[:, c, :], r=["wlg", ("xcb", c)], w=[pk(b2)])
                            P.act(rr[:, c, :], ps[b1][:, :], AF.Tanh, r=[pk(b1), "lrubh"], w=[("rr", c)],
                                  scale=0.5, bias=lrubh[:, d, 0, c:c + 1])
                            P.act(ii[:, c, :], ps[b2][:, :], AF.Tanh, r=[pk(b2), "lrubh"], w=[("ii", c)],
                                  scale=0.5, bias=lrubh[:, d, 1, c:c + 1])

                        if d == 0:
                            conv(cs[0])
                            conv(cs[1])
                            gates(cs[0])
                            conv(cs[2])
                            gates(cs[1])
                            conv(cs[3])
                            gates(cs[2])
                            gates(cs[3])
                            c0 = 4 * half
                            P.dma("pool", XC_d[g][:, c0 * T:(c0 + 4) * T], xc[:, c0:c0 + 4, :].rearrange("p c t -> p (c t)"),
                                  "st_xc%d" % half, r=[("xc", c) for c in cs], w=[("XC", g, half)])
                            P.dma("pool", XCB_d[g][:, c0 * T:(c0 + 4) * T], xcb[:, c0:c0 + 4, :].rearrange("p c t -> p (c t)"),
                                  "st_xcb%d" % half, r=[("xcb", c) for c in cs], w=[("XCB", g, half)])
                        else:
                            for c in cs:
                                gates(c)
                        for c in cs:
                            P.act(a2[:, c, :], rr[:, c, :], AF.Exp, r=[("rr", c), "nsp"], w=[("a2", c)],
                                  scale=nsp[:, d, c:c + 1], bias=nsp[:, d, c:c + 1])
                            P.act(rr[:, c, :], rr[:, c, :], AF.Exp, r=[("rr", c), "nsp"], w=[("rr", c)],
                                  scale=nsp2[:, d, c:c + 1], bias=nsp2[:, d, c:c + 1])
                        P.act(a2[:, cs[0]:cs[0] + 4, :], a2[:, cs[0]:cs[0] + 4, :], AF.Sqrt,
                              r=[("a2", c) for c in cs] + ["quart"], w=[("a2", c) for c in cs], scale=-0.25,
                              bias=quart[:, 0:1])

                    def stage_b(ti, t0, g, half, d=d):
                        cs = list(range(4 * half, 4 * half + 4))
                        if d == 1 and half == 0:
                            P.dma("sp", hfl[:].rearrange("p c t -> p (c t)"), HF_d[g], "ld_hf", r=[("HF", g)],
                                  w=[("hfl", c) for c in range(8)])
                            P.dma("sp", gyl[:].rearrange("p c t -> p (c t)"), GY_d[g], "ld_gy",
                                  w=[("gyl", c) for c in range(8)])
                        for c in cs:
                            P.stt("dve", ii[:, c, :], ii[:, c, :], 1.0, a2[:, c, :], ALU.add, ALU.mult,
                                  r=[("ii", c), ("a2", c)], w=[("ii", c)])
                            P.tt("pool", ii[:, c, :], ii[:, c, :], xc[:, c, :], ALU.mult, r=[("ii", c), ("xc", c)], w=[("ii", c)])
                        for c in cs:
                            if d == 0:
                                P.scan("dve", hh[:, c, :], rr[:, c, :], ii[:, c, :], state[:, c:c + 1],
                                       r=[("rr", c), ("ii", c), ("state", c)], w=[("hh", c)])
                                P.copy("dve", state[:, c:c + 1], hh[:, c, T - 1:T], r=[("hh", c)], w=[("state", c)])
                            else:
                                P.scan("dve", rev(hh[:, c, :]), rev(rr[:, c, :]), rev(ii[:, c, :]), state[:, c:c + 1],
                                       r=[("rr", c), ("ii", c), ("state", c)], w=[("hh", c)])
                                P.copy("dve", state[:, c:c + 1], hh[:, c, 0:1], r=[("hh", c)], w=[("state", c)])
                                P.tt("pool", hh[:, c, :], hh[:, c, :], hfl[:, c, :], ALU.add, r=[("hh", c), ("hfl", c)], w=[("hh", c)])
                        if d == 1:
                            for c in cs:
                                P.tt("dve", hy[:, c, :], hh[:, c, :], gyl[:, c, :], ALU.mult, r=[("hh", c), ("gyl", c)], w=[("hy", c)])
                        if half == 1:
                            if d == 0:
                                P.dma("pool", HF_d[g], hh[:].rearrange("p c t -> p (c t)"), "st_hf",
                                      r=[("hh", c) for c in range(8)], w=[("HF", g)])
                            else:
                                P.dma("pool", HY_d[g], hy[:].rearrange("p c t -> p (c t)"), "st_hy",
                                      r=[("hy", c) for c in range(8)], w=[("HY", g)])

                    groups = [(ti, t0, g, half) for (_, ti, t0, g) in order for half in (0, 1)]
                    stage_a(*groups[0])
                    for gi, grp in enumerate(groups):
                        if gi + 1 < len(groups):
                            stage_a(*groups[gi + 1])
                        stage_b(*grp)
            P.flush()
            if "B" not in phases:
                return nc

        SMAX = max(seq_lens)
        QSCALE = 96.0 ** -0.5
        with contextlib.ExitStack() as sbb:
            KT_s = sb("KT_s", [128, NH, SMAX], BF16, sbb)
            V_s = sb("V_s", [128, SMAX // 128, NH, 128], BF16, sbb)
            qtr = Ring("qt", [sb("qt%d" % i, [128, NH * T], BF16, sbb) for i in range(2)])
            ptr = Ring("pt", [sb("pt%d" % i, [128, 2, T], BF16, sbb) for i in range(4)])
            ou = [sb("ou%d" % i, [128, T], F32, sbb) for i in range(2)]
            rs = sb("rs", [128, T], F32, sbb)
            ostr = Ring("ost", [sb("ost%d" % i, [128, (NH // 2) * T], BF16, sbb) for i in range(2)])
            vkeys_all = [("V_s", i) for i in range(SMAX // T)] + ["V_ones"]
            P.memset("dve", V_s[:].rearrange("p a h e -> p (a h e)"), 0.0, w=vkeys_all)
            V_sp = V_s[:].rearrange("p a (j two) e -> p a j two e", two=2)
            P.memset("dve", V_sp[:, :, :, 0, 64:65], 1.0, w=vkeys_all)
            P.memset("dve", V_sp[:, :, :, 1, 0:1], 1.0, w=vkeys_all)
            P.memset("pool", KT_s[96:128, :, :].rearrange("p h t -> p (h t)"), 0.0, w=["K_pad"])
            for qb_ in qtr.bufs:
                P.memset("pool", qb_[96:128, :], 0.0, w=["Q_pad"])
            for s, S in enumerate(seq_lens):
                stiles = [tl for tl in tiles if tl[0] == s]
                nkc = S // 128
                for (_, ti, t0, g) in stiles:
                    P.dma("sp", KT_s[0:96, :, t0:t0 + T], KT_d[g].rearrange("p (h t) -> p h t", h=NH), "ld_kt",
                          w=[("KT_s", ti)])
                    vsrc = V_d[g].rearrange("p (a j two e) -> p a j two e", a=4, j=NH // 2, two=2)
                    P.dma("sp", V_sp[:, ti * 4:(ti + 1) * 4, :, 0, 0:64], vsrc[:, :, :, 0, :], "ld_v", w=[("V_s", ti)])
                    P.dma("sp", V_sp[:, ti * 4:(ti + 1) * 4, :, 1, 64:128], vsrc[:, :, :, 1, :], "ld_v", w=[("V_s", ti)])
                kall = [("KT_s", i) for i in range(len(stiles))] + ["K_pad", "Q_pad"]
                vall = [("V_s", i) for i in range(len(stiles))] + ["V_ones"]
                for (_, ti, t0, g) in stiles:
                    qb, qkey = qtr.next()
                    P.dma("sp", qb[0:96, :], QT_d[g], qkey, w=[qkey])
                    qv = qb[:, :].rearrange("p (h t) -> p h t", h=NH)
                    osb, okey = ostr.next()
                    ov = osb[:, :].rearrange("p (j t) -> p j t", j=NH // 2)
                    def finalize(h):
                        bo = 6
                        oub = ou[h % 2]
                        ouk = ("ou", h % 2)
                        if h % 2 == 0:
                            P.copy("dve", oub[0:65, :], ps[bo][0:65, :], r=[pk(bo)], w=[ouk])
                            P.mm(ps[7][0:64, :], ones_f[64:65, 0:64], oub[64:65, :], r=[ouk, "ones_f"], w=[pk(7)])
                            P.recip(rs[0:64, :], ps[7][0:64, :], r=[pk(7)], w=["rs"])
                            P.tt("dve", ov[0:64, h // 2, :], oub[0:64, :], rs[0:64, :], ALU.mult, r=[ouk, "rs"],
                                 w=[(okey, h)])
                        else:
                            P.copy("dve", oub[:, :], ps[bo][:, :], r=[pk(bo)], w=[ouk])
                            P.mm(ps[7][:, :], ones_f[0:1, :], oub[0:1, :], r=[ouk, "ones_f"], w=[pk(7)])
                            P.recip(rs[64:128, :], ps[7][64:128, :], r=[pk(7)], w=["rs"])
                            P.tt("dve", ov[64:128, h // 2, :], oub[64:128, :], rs[64:128, :], ALU.mult, r=[ouk, "rs"],
                                 w=[(okey, h)])

                    for h in range(NH):
                        npair = nkc // 2
                        bo = 6

                        def qk(j):
                            for i in range(2):
                                kc = 2 * j + i
                                b = (j % 3) * 2 + i
                                P.mm(ps[b][:, :], KT_s[:, h, kc * 128:(kc + 1) * 128], qv[:, h, :],
                                     r=[("KT_s", kc // 4), "K_pad", "Q_pad", qkey], w=[pk(b)])

                        qk(0)
                        qk(1)
                        if h > 0:
                            finalize(h - 1)
                        for j in range(npair):
                            if j + 2 < npair:
                                qk(j + 2)
                            pb, pkey = ptr.next()
                            pj = j % 3
                            P.act(pb[:, :, :], pp[pj][:, :].rearrange("p (a t) -> p a t", a=2), AF.Exp,
                                  r=[pk(2 * pj), pk(2 * pj + 1)], w=[pkey], scale=QSCALE)
                            for i in range(2):
                                kc = 2 * j + i
                                P.mm(ps[bo][:, :], V_s[:, kc, h, :], pb[:, i, :], start=(kc == 0), stop=(kc == nkc - 1),
                                     r=[("V_s", kc // 4), pkey], w=[pk(bo)])
                    finalize(NH - 1)
                    P.dma("pool", OT_d[g], osb[:, :], "st_ot", r=[(okey, h) for h in range(NH)], w=[("OT", g)])
            P.flush()
            if "C" not in phases:
                return nc

        with contextlib.ExitStack() as scc:
            xT = sb("xT_c", [128, 8, T], F32, scc)
            sqm = sb("sqm_c", [128, 8, T], BF16, scc)
            sqf = sb("sqf_c", [128, 8, T], BF16, scc)
            hT = sb("hT_c", [128, 8, T], BF16, scc)
            actT = sb("actT_c", [128, NOC_FF, T], BF16, scc)
            fTm = sb("fTm_c", [128, 8, T], F32, scc)
            fTf = sb("fTf_c", [128, 8, T], F32, scc)
            silu_t = [sb("silu_c%d" % i, [128, T], F32, scc) for i in range(2)]
            tmpf = [sb("tmpf_c%d" % i, [128, T], F32, scc) for i in range(2)]
            rstd = sb("rstd_c", [128, T], F32, scc)
            otl = sb("otl", [128, (NH // 2) * T], BF16, scc)
            hyl = sb("hyl", [128, 8 * T], BF16, scc)
            sal = sb("sal", [128, 8 * T], BF16, scc)
            sll = sb("sll", [128, 8 * T], BF16, scc)
            mg = sb("mg", [128, 8, T], BF16, scc)
            wao_s = sb("wao_s", [128, (NH // 2) * 1024], BF16, scc)
            yout = sb("yout", [128, 4, D], F32, scc)
            wring = Ring("wring_c", [sb("wring_c%d" % i, [128, 4096], BF16, scc) for i in range(3)])
            P.dma("sp", wao_s[:, :], wao_v.ap, "c_wao", r=wao_v.keys(), w=["wao"])
            waov = wao_s[:, :].rearrange("p (j n) -> p j n", j=NH // 2)
            otv = otl[:, :].rearrange("p (j t) -> p j t", j=NH // 2)
            hyv = hyl[:, :].rearrange("p (c t) -> p c t", c=8)
            sav = sal[:, :].rearrange("p (c t) -> p c t", c=8)
            slv = sll[:, :].rearrange("p (c t) -> p c t", c=8)
            xkeys = [("xT", k) for k in range(8)]

            def stage_M(tile):
                s, ti, t0, g = tile
                P.dma("sp", otl[:, :], OT_d[g], "ld_ot", w=["otl"])
                P.dma("sp", hyl[:, :], HY_d[g], "ld_hy", w=["hyl"])
                P.dma("sp", sal[:, :], SA_d[g], "ld_sa", w=["sal"])
                P.dma("sp", sll[:, :], SL_d[g], "ld_sl", w=["sll"])
                for grp in range(2):
                    slot, skey = wring.next()
                    P.dma("sp", slot[:, :], wlo_v[grp], skey, r=wlo_v.keys(grp), w=[skey])
                    sv = slot[:, :].rearrange("p (j k n) -> p j k n", j=4, k=8)
                    for j in range(4):
                        oc = grp * 4 + j
                        ba, bl = (0, 1) if oc % 2 == 0 else (2, 3)
                        for jp in range(NH // 2):
                            P.mm(ps[ba][:, :], waov[:, jp, oc * 128:(oc + 1) * 128], otv[:, jp, :],
                                 start=(jp == 0), stop=(jp == NH // 2 - 1), r=["wao", "otl"], w=[pk(ba)])
                        for kc in range(8):
                            P.mm(ps[bl][:, :], sv[:, j, kc, :], hyv[:, kc, :], start=(kc == 0), stop=(kc == 7),
                                 r=[skey, "hyl"], w=[pk(bl)])
                        P.tt("dve", tmpf[0][:], ps[ba][:, :], sav[:, oc, :], ALU.mult, r=[pk(ba), "sal"], w=[("tmpf", 0)])
                        P.tt("dve", tmpf[1][:], ps[bl][:, :], slv[:, oc, :], ALU.mult, r=[pk(bl), "sll"], w=[("tmpf", 1)])
                        P.tt("pool", mg[:, oc, :], tmpf[0][:], tmpf[1][:], ALU.add, r=[("tmpf", 0), ("tmpf", 1)],
                             w=[("mg", oc)])
                mk = [("mg", k) for k in range(8)]
                for grp in range(2):
                    slot, skey = wring.next()
                    P.dma("sp", slot[:, :], wout_v[grp], skey, r=wout_v.keys(grp), w=[skey])
                    sv = slot[:, :].rearrange("p (j k n) -> p j k n", j=4, k=8)
                    for j in range(4):
                        oc = grp * 4 + j
                        b = 4 + oc % 2
                        for kc in range(8):
                            P.mm(ps[b][:, :], sv[:, j, kc, :], mg[:, kc, :], start=(kc == 0), stop=(kc == 7),
                                 r=[skey] + mk, w=[pk(b)])
                        P.copy("dve", fTm[:, oc, :], ps[b][:, :], r=[pk(b)], w=[("fTm", oc)])
                        P.act(sqm[:, oc, :], fTm[:, oc, :], AF.Square, r=[("fTm", oc)], w=[("sqm", oc)])

            def gen_N1c(tile):
                s, ti, t0, g = tile
                P.dma("sp", xT[:].rearrange("p c t -> p (c t)"), X1_d[g], "ld_x1", r=[("X1", g)], w=xkeys)
                rstd_from_sq(lambda i: sqm[:, i, :], 8, D, rstd, [("sqm", i) for i in range(8)])
                yield
                for fc in range(8):
                    tb = tmpf[fc % 2]
                    tk = ("tmpf", fc % 2)
                    P.tt("dve", tb[:], fTm[:, fc, :], rstd[:], ALU.mult, r=[("fTm", fc), "rstd"], w=[tk])
                    P.stt("dve", xT[:, fc, :], tb[:], modp[:, s, 1, 2, fc:fc + 1], xT[:, fc, :], ALU.mult, ALU.add,
                          r=[tk, "modp", ("xT", fc)], w=[("xT", fc)])
                    P.act(sqm[:, fc, :], xT[:, fc, :], AF.Square, r=[("xT", fc)], w=[("sqm", fc)])
                    if fc % 2 == 1:
                        yield
                P.dma("pool", X1_d[g], xT[:].rearrange("p c t -> p (c t)"), "st_x2", r=xkeys, w=[("X1", g)])
                rstd_from_sq(lambda i: sqm[:, i, :], 8, D, rstd, [("sqm", i) for i in range(8)])
                yield
                for fc in range(8):
                    tb = tmpf[fc % 2]
                    tk = ("tmpf", fc % 2)
                    P.tt("dve", tb[:], xT[:, fc, :], rstd[:], ALU.mult, r=[("xT", fc), "rstd"], w=[tk])
                    P.act(hT[:, fc, :], tb[:], AF.Identity, r=[tk, "modp"], w=[("hT", fc)],
                          scale=modp[:, s, 2, 0, fc:fc + 1], bias=modp[:, s, 2, 1, fc:fc + 1])
                    if fc % 4 == 3:
                        yield

            def gen_N2(tile):
                s, ti, t0, g = tile
                tok0 = g * T
                P.dma("sp", xT[:].rearrange("p c t -> p (c t)"), X1_d[g], "ld_x1", r=[("X1", g)], w=xkeys)
                rstd_from_sq(lambda i: sqf[:, i, :], 8, D, rstd, [("sqf", i) for i in range(8)])
                yield
                for fc in range(8):
                    tb = tmpf[fc % 2]
                    tk = ("tmpf", fc % 2)
                    P.tt("dve", tb[:], fTf[:, fc, :], rstd[:], ALU.mult, r=[("fTf", fc), "rstd"], w=[tk])
                    P.stt("dve", xT[:, fc, :], tb[:], modp[:, s, 2, 2, fc:fc + 1], xT[:, fc, :], ALU.mult, ALU.add,
                          r=[tk, "modp", ("xT", fc)], w=[("xT", fc)])
                    yield
                for a in range(4):
                    for half in range(2):
                        for q in range(4):
                            fc = half * 4 + q
                            P.tr(ps[7][:, q * 128:(q + 1) * 128], xT[:, fc, a * 128:(a + 1) * 128], ident[:],
                                 r=[("xT", fc), "ident"], w=[pk(7)])
                        dst = yout[:, a, half * 512:(half + 1) * 512]
                        P.copy("act", dst, ps[7][:, :], r=[pk(7)], w=[("yout", a, half)])
                        yield
                P.dma("pool", y_d[tok0:tok0 + T, :].rearrange("(a p) d -> p a d", p=128), yout[:], "st_y",
                      r=[("yout", a, hf_) for a in range(4) for hf_ in range(2)], w=[("Y", g)])

            def run(gen):
                for _ in gen:
                    pass

            def interleave(main, side):
                live = side is not None
                for _ in main:
                    if live:
                        try:
                            next(side)
                        except StopIteration:
                            live = False
                if live:
                    run(side)

            ntl = len(tiles)
            for i in range(ntl + 1):
                if i < ntl:
                    stage_M(tiles[i])
                interleave(gen_ffn_down(actT, fTf, sqf, wring, w2dn_v, "fTf", "sqf") if i > 0 else iter(()),
                           gen_N1c(tiles[i]) if i < ntl else None)
                interleave(gen_ffn_up(hT, actT, silu_t, wring, w2gu_v) if i < ntl else iter(()),
                           gen_N2(tiles[i - 1]) if i > 0 else None)
            P.flush(final=True)
    return nc


SEQ_LENS = (2048, 2048, 4096, 4096)
_CACHE = {}


def kernel(**inp):
    inp = {k: np.asarray(v) for k, v in inp.items()}
    wbig, woffs = host_weights(inp)
    sm = host_small(inp)
    key = ("nc", wbig.size)
    if key not in _CACHE:
        _CACHE[key] = build(SEQ_LENS, wbig.size, woffs)
    nc = _CACHE[key]
    wbig3 = wbig.reshape(-1, 128, W_CH // 128)
    xp, xs = inp["x_prompt"], inp["x_sample"]
    cp, cs = inp["c_prompt"], inp["c_sample"]
    in_maps = []
    for c in range(N_CORES):
        x = np.concatenate([xp[2 * c].reshape(-1, D), xp[2 * c + 1].reshape(-1, D),
                            xs[2 * c].reshape(-1, D), xs[2 * c + 1].reshape(-1, D)], axis=0)
        cv = np.stack([cp[2 * c], cp[2 * c + 1], cs[2 * c], cs[2 * c + 1]], axis=0)
        cT = np.ascontiguousarray(cv.reshape(4, 8, 128).transpose(2, 1, 0))
        m = {"x": np.ascontiguousarray(x, dtype=np.float32), "cT": cT, "wbig": wbig3}
        m.update(sm)
        in_maps.append(m)
    res = run_bass_kernel_spmd(nc, in_maps, core_ids=list(range(N_CORES)))
    yp = np.empty_like(xp)
    ys = np.empty_like(xs)
    for c in range(N_CORES):
        y = res.results[c]["y"]
        yp[2 * c] = y[0:2048]
        yp[2 * c + 1] = y[2048:4096]
        ys[2 * c] = y[4096:8192]
        ys[2 * c + 1] = y[8192:12288]
    return yp, ys
```
